# Optimizing a Trainium2 kernel written in Bass

```python
import math
import jax, jax.numpy as jnp
from jax import lax
import numpy as np

D_MODEL = 4096
BATCH = 1
SEQ = 8192
DEPTH = 2

CHUNK = 64
N_MIXERS = 2
HGRN_EXPAND = 128
HGRN_HEADS = D_MODEL // HGRN_EXPAND
HGRN_HEAD_V = D_MODEL // HGRN_HEADS
SB_HEADS = 32
SB_HEAD_DIM = D_MODEL // SB_HEADS
Q_BLOCK = 128
D_FF = ((8 * D_MODEL // 3 + 255) // 256) * 256
N_HGRN_LAYERS = (DEPTH + 1) // 2
N_SB_LAYERS = DEPTH // 2
EPS = 1e-6

kernel_name = 'hybrid_hgrn2_stickbreaking_macaron'


def rmsnorm(x, w):
    x32 = x.astype(jnp.float32)
    y = x32 * lax.rsqrt(jnp.mean(x32 * x32, axis=-1, keepdims=True) + EPS)
    return (y * w.astype(jnp.float32)).astype(x.dtype)


def swiglu(u, w_gu, w_down):
    g, v = jnp.split(u @ w_gu, 2, axis=-1)
    return (jax.nn.silu(g) * v) @ w_down


def gla_chunk_scan(q, k, v, g):
    B, S, H, K = q.shape
    V = v.shape[-1]
    n_chunks = S // CHUNK

    def to_chunks(a):
        return a.reshape(B, n_chunks, CHUNK, H, a.shape[-1]).transpose(1, 0, 2, 3, 4)

    causal = jnp.tril(jnp.ones((CHUNK, CHUNK), dtype=bool))

    def step(state, inp):
        qc, kc, vc, gc = inp
        b = jnp.cumsum(gc, axis=1)
        o_inter = jnp.einsum('bthk,bhkv->bthv', qc * jnp.exp(b), state)
        diff = b[:, :, None] - b[:, None, :]
        decay = jnp.exp(jnp.where(causal[None, :, :, None, None], diff, -jnp.inf))
        attn = jnp.einsum('bthk,bshk,btshk->bhts', qc, kc, decay)
        o_intra = jnp.einsum('bhts,bshv->bthv', attn, vc)
        b_last = b[:, -1]
        k_dec = kc * jnp.exp(b_last[:, None] - b)
        new_state = jnp.exp(b_last)[..., None] * state + jnp.einsum('bshk,bshv->bhkv', k_dec, vc)
        return new_state, o_inter + o_intra

    state0 = jnp.zeros((B, H, K, V), jnp.float32)
    _, o = lax.scan(step, state0, (to_chunks(q), to_chunks(k), to_chunks(v), to_chunks(g)))
    return o.transpose(1, 0, 2, 3, 4).reshape(B, S, H, V)


def hgrn2_mixer(u, w_in, lb, g_norm, w_out):
    B, S, _ = u.shape
    q, f, i_in, gate = jnp.split(u @ w_in, 4, axis=-1)

    def heads(a):
        return a.reshape(B, S, HGRN_HEADS, -1).astype(jnp.float32)

    lb_h = lb.reshape(HGRN_HEADS, HGRN_EXPAND)
    q = jax.nn.silu(heads(q))
    f = lb_h + (1.0 - lb_h) * jax.nn.sigmoid(heads(f))
    o = gla_chunk_scan(q, 1.0 - f, heads(i_in), jnp.log(f))
    o = rmsnorm(o, g_norm) * jax.nn.silu(heads(gate))
    return o.reshape(B, S, D_MODEL).astype(u.dtype) @ w_out


def stick_breaking_mixer(u, w_in, w_out):
    B, S, _ = u.shape
    q, k, v = jnp.split(u @ w_in, 3, axis=-1)

    def heads(a):
        return a.reshape(B, S, SB_HEADS, SB_HEAD_DIM).transpose(0, 2, 1, 3).astype(jnp.float32)

    q, k, v = heads(q), heads(k), heads(v)
    n_blocks = S // Q_BLOCK
    q_blocks = q.reshape(B, SB_HEADS, n_blocks, Q_BLOCK, SB_HEAD_DIM).transpose(2, 0, 1, 3, 4)
    key_pos = jnp.arange(S)
    scale = 1.0 / math.sqrt(SB_HEAD_DIM)

    def block(args):
        q_blk, blk = args
        q_pos = blk * Q_BLOCK + jnp.arange(Q_BLOCK)
        strict = key_pos[None, :] < q_pos[:, None]
        z = jnp.einsum('bhqd,bhsd->bhqs', q_blk, k) * scale
        log_keep = jnp.where(strict, jax.nn.log_sigmoid(-z), 0.0)
        rev = lax.cumsum(log_keep, axis=3, reverse=True)
        log_survive = jnp.concatenate([rev[..., 1:], jnp.zeros_like(rev[..., :1])], axis=-1)
        a = jnp.where(strict, jnp.exp(jax.nn.log_sigmoid(z) + log_survive), 0.0)
        return jnp.einsum('bhqs,bhsd->bhqd', a, v)

    o = lax.map(block, (q_blocks, jnp.arange(n_blocks)))
    o = o.transpose(1, 0, 3, 2, 4).reshape(B, S, D_MODEL)
    return o.astype(u.dtype) @ w_out


def setup_inputs(seed: int = 0) -> dict:
    key = jax.random.key(seed)
    ks = jax.random.split(key, 16)
    nrm = jax.random.normal
    f32 = jnp.float32
    s_d = D_MODEL ** -0.5
    s_f = D_FF ** -0.5
    return {
        'x': nrm(ks[0], (BATCH, SEQ, D_MODEL), f32),
        'ffn1_norm': 1.0 + 0.01 * nrm(ks[1], (DEPTH, D_MODEL), f32),
        'ffn1_w_gu': nrm(ks[2], (DEPTH, D_MODEL, 2 * D_FF), f32) * s_d,
        'ffn1_w_down': nrm(ks[3], (DEPTH, D_FF, D_MODEL), f32) * s_f,
        'mix_norm': 1.0 + 0.01 * nrm(ks[4], (DEPTH, D_MODEL), f32),
        'ffn2_norm': 1.0 + 0.01 * nrm(ks[5], (DEPTH, D_MODEL), f32),
        'ffn2_w_gu': nrm(ks[6], (DEPTH, D_MODEL, 2 * D_FF), f32) * s_d,
        'ffn2_w_down': nrm(ks[7], (DEPTH, D_FF, D_MODEL), f32) * s_f,
        'hgrn_w_in': nrm(ks[8], (N_HGRN_LAYERS, D_MODEL, 4 * D_MODEL), f32) * s_d,
        'hgrn_lb_logits': 0.1 * nrm(ks[9], (DEPTH + 1, D_MODEL), f32),
        'hgrn_gnorm': 1.0 + 0.01 * nrm(ks[10], (N_HGRN_LAYERS, HGRN_HEAD_V), f32),
        'hgrn_w_out': nrm(ks[11], (N_HGRN_LAYERS, D_MODEL, D_MODEL), f32) * s_d,
        'sb_w_in': nrm(ks[12], (N_SB_LAYERS, D_MODEL, 3 * D_MODEL), f32) * s_d,
        'sb_w_out': nrm(ks[13], (N_SB_LAYERS, D_MODEL, D_MODEL), f32) * s_d,
        'final_norm': 1.0 + 0.01 * nrm(ks[14], (D_MODEL,), f32),
    }


def reference(x, ffn1_norm, ffn1_w_gu, ffn1_w_down, mix_norm, ffn2_norm, ffn2_w_gu, ffn2_w_down,
              hgrn_w_in, hgrn_lb_logits, hgrn_gnorm, hgrn_w_out, sb_w_in, sb_w_out, final_norm):
    lb_table = jnp.cumsum(jax.nn.softmax(hgrn_lb_logits.astype(jnp.float32), axis=0), axis=0)
    h = x
    for i in range(DEPTH):
        h = h + 0.5 * swiglu(rmsnorm(h, ffn1_norm[i]), ffn1_w_gu[i], ffn1_w_down[i])
        u = rmsnorm(h, mix_norm[i])
        j = i // N_MIXERS
        if i % N_MIXERS == 0:
            h = h + hgrn2_mixer(u, hgrn_w_in[j], lb_table[i], hgrn_gnorm[j], hgrn_w_out[j])
        else:
            h = h + stick_breaking_mixer(u, sb_w_in[j], sb_w_out[j])
        h = h + 0.5 * swiglu(rmsnorm(h, ffn2_norm[i]), ffn2_w_gu[i], ffn2_w_down[i])
    return rmsnorm(h, final_norm)
```

```python
import contextlib
import math
import numpy as np
import ml_dtypes
import concourse.bass as bass
import concourse.mybir as mybir
from concourse.bass_utils import run_bass_kernel_spmd

F32 = mybir.dt.float32
BF16 = mybir.dt.bfloat16
AF = mybir.ActivationFunctionType
ALU = mybir.AluOpType
NPBF = ml_dtypes.bfloat16

EPS = 1e-6
TT = 512


class Buf:
    __slots__ = ("w", "r", "dsem", "dcnt", "name", "dkey")

    def __init__(self, name=""):
        self.w = {}
        self.r = {}
        self.dsem = None
        self.dcnt = 0
        self.name = name
        self.dkey = None


class Sched:
    def __init__(self, nc, stack):
        self.nc = nc
        self.stack = stack
        self.ops = {k: [] for k in ("pe", "act", "dve", "pool", "sp")}
        self.sem = {}
        self.cnt = {}
        for k in ("pe", "act", "dve", "pool"):
            self.sem[k] = stack.enter_context(nc.semaphore("s_" + k))
            self.cnt[k] = 0
        self.known = {k: {} for k in self.ops}
        self.ndsem = 0
        self.free_dsems = []
        self.stage_bufs = []

    def _deps(self, e, reads, writes):
        deps = {}
        for b in reads:
            for k, ev in b.w.items():
                if k not in deps or deps[k][1] < ev[1]:
                    deps[k] = ev
        for b in writes:
            for d in (b.w, b.r):
                for k, ev in d.items():
                    if k not in deps or deps[k][1] < ev[1]:
                        deps[k] = ev
        if e == "pe":
            deps.pop("pe", None)
        kn = self.known[e]
        for k, (sem, val) in deps.items():
            if kn.get(k, 0) < val:
                kn[k] = val
                self.ops[e].append(lambda eng, sem=sem, val=val: eng.wait_ge(sem, val))

    def op(self, e, fn, reads=(), writes=()):
        self._deps(e, reads, writes)
        self.cnt[e] += 1
        sem = self.sem[e]
        self.ops[e].append(lambda eng, fn=fn, sem=sem: fn(eng).then_inc(sem, 1))
        ev = (sem, self.cnt[e])
        for b in reads:
            b.r[e] = ev
        for b in writes:
            b.w[e] = ev
            b.r = {}

    def dma(self, q, out_ap, in_ap, reads=(), writes=(), local=True, **kw):
        self._deps(q, reads, writes)
        b = writes[0]
        if b.dsem is None:
            if self.free_dsems:
                b.dsem, b.dkey, b.dcnt = self.free_dsems.pop()
            else:
                b.dsem = self.stack.enter_context(self.nc.semaphore("d%d" % self.ndsem))
                b.dkey = "d%d" % self.ndsem
                self.ndsem += 1
            if local:
                self.stage_bufs.append(b)
        b.dcnt += 16
        dsem = b.dsem
        ev = (dsem, b.dcnt)
        key = b.dkey
        self.ops[q].append(
            lambda eng, o=out_ap, i=in_ap, kw=kw, dsem=dsem:
            eng.dma_start(out=o, in_=i, **kw).then_inc(dsem, 16))
        for rb in reads:
            rb.r[key] = ev
        for wb in writes:
            wb.w[key] = ev
            wb.r = {}

    def wait_all(self, e, bufs):
        self._deps(e, bufs, ())

    def emit(self):
        nc = self.nc
        ops = self.ops
        with nc.Block() as block:
            @block.sync
            def _(eng):
                for f in ops["sp"]:
                    f(eng)

            @block.scalar
            def _(eng):
                for f in ops["act"]:
                    f(eng)

            @block.vector
            def _(eng):
                for f in ops["dve"]:
                    f(eng)

            @block.gpsimd
            def _(eng):
                for f in ops["pool"]:
                    f(eng)

            @block.tensor
            def _(eng):
                for f in ops["pe"]:
                    f(eng)
        self.ops = {k: [] for k in ops}
        for b in self.stage_bufs:
            self.free_dsems.append((b.dsem, b.dkey, b.dcnt))
            b.dsem = None
        self.stage_bufs = []


_UID = [0]


def sbt(nc, st, name, shape, dtype):
    _UID[0] += 1
    return st.enter_context(nc.sbuf_tensor("%s_%d" % (name, _UID[0]), list(shape), dtype))


def pst(nc, st, name, shape, dtype):
    _UID[0] += 1
    return st.enter_context(nc.psum_tensor("%s_%d" % (name, _UID[0]), list(shape), dtype))


class Ring:
    def __init__(self, nc, st, name, n, shape, dtype):
        self.t = [sbt(nc, st, "%s%d" % (name, i), shape, dtype) for i in range(n)]
        self.b = [Buf("%s%d" % (name, i)) for i in range(n)]
        self.n = n
        self.i = 0

    def next(self):
        k = self.i % self.n
        self.i += 1
        return self.t[k], self.b[k]


class PRing:
    def __init__(self, nc, st, name, n, shape=(128, 512), dtype=F32):
        self.t = [pst(nc, st, "%s%d" % (name, i), list(shape), dtype) for i in range(n)]
        self.b = [Buf("%s%d" % (name, i)) for i in range(n)]
        self.n = n
        self.i = 0

    def next(self):
        k = self.i % self.n
        self.i += 1
        return self.t[k], self.b[k]


class Consts:
    def __init__(self, nc, st, S, cdram):
        self.ones = sbt(nc, st, "c_ones", [128, 128], F32)
        self.ident = sbt(nc, st, "c_ident", [128, 128], F32)
        self.eps = sbt(nc, st, "c_eps", [128, 1], F32)
        self.b = Buf("consts")
        S.dma("sp", self.ones[:], cdram["ones"], writes=[self.b])
        S.dma("sp", self.ident[:], cdram["ident"], writes=[self.b])
        S.dma("sp", self.eps[:], cdram["eps"], writes=[self.b])


def load_cols(nc, st, S, name, vec_ap, nch):
    t = sbt(nc, st, name, [128, nch], F32)
    b = Buf(name)
    S.dma("sp", t[:], vec_ap.rearrange("(c p) -> p c", p=128), writes=[b], allow_slow_non_contiguous=True)
    return t, b


def emit_transpose_in(nc, S, st, C, x_tok, hT, hTb, ntok, D):
    DC = D // 128
    xin = Ring(nc, st, "ti_x", 2, [128, D], F32)
    xo = Ring(nc, st, "ti_o", 2, [128, DC, 128], F32)
    ps = PRing(nc, st, "ti_ps", 4)
    for tb in range(ntok // 128):
        xt, xb = xin.next()
        S.dma("sp", xt[:], x_tok[tb * 128:(tb + 1) * 128, :], writes=[xb])
        ot, ob = xo.next()
        for g in range(DC // 4):
            pt, pb = ps.next()
            for a in range(4):
                dc = g * 4 + a
                S.op("pe", lambda e, pt=pt, xt=xt, a=a, dc=dc: e.transpose(
                    out=pt[:, a * 128:(a + 1) * 128], in_=xt[:, dc * 128:(dc + 1) * 128], identity=C.ident[:]),
                    reads=[xb, C.b], writes=[pb])
            eng = "act" if g % 2 == 0 else "dve"
            if eng == "act":
                S.op("act", lambda e, pt=pt, ot=ot, g=g: e.copy(
                    out=ot[:, g * 4:(g + 1) * 4, :], in_=pt[:].rearrange("p (a t) -> p a t", a=4)),
                    reads=[pb], writes=[ob])
            else:
                S.op("dve", lambda e, pt=pt, ot=ot, g=g: e.tensor_copy(
                    out=ot[:, g * 4:(g + 1) * 4, :], in_=pt[:].rearrange("p (a t) -> p a t", a=4)),
                    reads=[pb], writes=[ob])
        S.dma("act", hT.rearrange("(c p) t -> p c t", p=128)[:, :, tb * 128:(tb + 1) * 128], ot[:],
              reads=[ob], writes=[hTb], local=False)


def emit_norm_to_uT(nc, S, st, C, hT, hTb, t0, uT, uTb, wcol, wcolb, rs, D, hb_ring, sq_ring, ss_ps, ss_b):
    DC = D // 128
    hv = hT.rearrange("(c p) t -> p c t", p=128)
    G = 2
    for g in range(DC // G):
        ht, hb = hb_ring.next()
        S.dma("sp", ht[:], hv[:, g * G:(g + 1) * G, t0:t0 + TT], reads=[hTb], writes=[hb])
        for a in range(G):
            dc = g * G + a
            sq, sqb = sq_ring.next()
            S.op("act", lambda e, sq=sq, ht=ht, a=a: e.activation(out=sq[:], in_=ht[:, a, :], func=AF.Square),
                 reads=[hb], writes=[sqb])
            S.op("pe", lambda e, sq=sq, dc=dc: e.matmul(ss_ps[:], lhsT=C.ones[:], rhs=sq[:],
                                                        start=(dc == 0), stop=(dc == DC - 1)),
                 reads=[sqb, C.b], writes=[ss_b])
    rt, rtb, rstd, rstdb = rs
    S.op("act", lambda e: e.activation(out=rt[:], in_=ss_ps[:], func=AF.Sqrt, bias=C.eps[:], scale=1.0 / D),
         reads=[ss_b, C.b], writes=[rtb])
    S.op("dve", lambda e: e.reciprocal(out=rstd[:], in_=rt[:]), reads=[rtb], writes=[rstdb])
    for g in range(DC // G):
        ht, hb = hb_ring.next()
        S.dma("sp", ht[:], hv[:, g * G:(g + 1) * G, t0:t0 + TT], reads=[hTb], writes=[hb])
        for a in range(G):
            dc = g * G + a
            S.op("dve", lambda e, ht=ht, a=a, dc=dc: e.scalar_tensor_tensor(
                out=uT[:, dc, :], in0=ht[:, a, :], scalar=wcol[:, dc:dc + 1], in1=rstd[:],
                op0=ALU.mult, op1=ALU.mult), reads=[hb, rstdb, wcolb], writes=[uTb[dc]])


def emit_ffn(nc, S, st, C, hT, hTb, ntok, D, DFF, nw, w_gu, w_down):
    DC = D // 128
    FC = DFF // 128
    JN = 4
    uT = sbt(nc, st, "f_uT", [128, DC, TT], BF16)
    uTb = [Buf("uT%d" % i) for i in range(DC)]
    actT = sbt(nc, st, "f_act", [128, FC, TT], BF16)
    actb = [Buf("act%d" % i) for i in range(FC)]
    wcol, wcolb = load_cols(nc, st, S, "f_nw", nw, DC)
    gu = Ring(nc, st, "f_gu", 4, [128, DC, 128], BF16)
    wd = Ring(nc, st, "f_wd", 3, [128, JN, 512], BF16)
    hb_ring = Ring(nc, st, "f_hb", 2, [128, 2, TT], F32)
    sq_ring = Ring(nc, st, "f_sq", 2, [128, TT], F32)
    rt = sbt(nc, st, "f_rt", [128, TT], F32)
    rstd = sbt(nc, st, "f_rstd", [128, TT], F32)
    rs = (rt, Buf("rt"), rstd, Buf("rstd"))
    sl_ring = Ring(nc, st, "f_sl", 2, [128, TT], F32)
    hres = Ring(nc, st, "f_hres", 2, [128, TT], F32)
    hnew = Ring(nc, st, "f_hnew", 2, [128, TT], F32)
    ss_ps = pst(nc, st, "f_ss", [128, TT], F32)
    ss_b = Buf("ss")
    pg = PRing(nc, st, "f_pg", 2)
    pv = PRing(nc, st, "f_pv", 1)
    po = PRing(nc, st, "f_po", 4)

    wgu_v = w_gu.rearrange("(c p) f -> p c f", p=128)
    wd_v = w_down.rearrange("(j p) d -> p j d", p=128)
    hv = hT.rearrange("(c p) t -> p c t", p=128)
    ntiles = ntok // TT
    NJB = (FC + JN - 1) // JN

    loads = []
    for tile in range(ntiles):
        for j in range(FC):
            loads.append(("g", j))
            loads.append(("u", j))
        for og in range(DC // 4):
            for jb in range(NJB):
                loads.append(("d", og, jb))
    slots = [None] * len(loads)
    state = {"next": 0}

    def prefetch(upto):
        while state["next"] < min(upto, len(loads)):
            i = state["next"]
            L = loads[i]
            if L[0] in ("g", "u"):
                t, b = gu.next()
                col = L[1] * 128 + (DFF if L[0] == "u" else 0)
                S.dma("pool", t[:], wgu_v[:, :, col:col + 128], writes=[b])
            else:
                t, b = wd.next()
                og, jb = L[1], L[2]
                jn = min(JN, FC - jb * JN)
                S.dma("pool", t[:, 0:jn, :], wd_v[:, jb * JN:jb * JN + jn, og * 512:(og + 1) * 512], writes=[b])
            slots[i] = (t, b)
            state["next"] += 1

    li = 0
    for tile in range(ntiles):
        t0 = tile * TT
        prefetch(li + 2)
        emit_norm_to_uT(nc, S, st, C, hT, hTb, t0, uT, uTb, wcol, wcolb, rs, D, hb_ring, sq_ring, ss_ps, ss_b)
        for j in range(FC):
            prefetch(li + 4)
            (wg, wgb), (wu, wub) = slots[li], slots[li + 1]
            li += 2
            pgt, pgb = pg.next()
            pvt, pvb = pv.next()
            for dc in range(DC):
                S.op("pe", lambda e, pgt=pgt, wg=wg, dc=dc: e.matmul(
                    pgt[:], lhsT=wg[:, dc, :], rhs=uT[:, dc, :], start=(dc == 0), stop=(dc == DC - 1)),
                    reads=[wgb, uTb[dc]], writes=[pgb])
            for dc in range(DC):
                S.op("pe", lambda e, pvt=pvt, wu=wu, dc=dc: e.matmul(
                    pvt[:], lhsT=wu[:, dc, :], rhs=uT[:, dc, :], start=(dc == 0), stop=(dc == DC - 1)),
                    reads=[wub, uTb[dc]], writes=[pvb])
            sl, slb = sl_ring.next()
            S.op("act", lambda e, sl=sl, pgt=pgt: e.activation(out=sl[:], in_=pgt[:], func=AF.Silu),
                 reads=[pgb], writes=[slb])
            S.op("dve", lambda e, sl=sl, pvt=pvt, j=j: e.tensor_tensor(
                out=actT[:, j, :], in0=pvt[:], in1=sl[:], op=ALU.mult), reads=[pvb, slb], writes=[actb[j]])
        for og in range(DC // 4):
            banks = [po.next() for _ in range(4)]
            for jb in range(NJB):
                prefetch(li + 3)
                wt, wb = slots[li]
                li += 1
                jn = min(JN, FC - jb * JN)
                for a in range(jn):
                    j = jb * JN + a
                    for q in range(4):
                        S.op("pe", lambda e, q=q, a=a, j=j, wt=wt, pt=banks[q][0]: e.matmul(
                            pt[:], lhsT=wt[:, a, q * 128:(q + 1) * 128], rhs=actT[:, j, :],
                            start=(j == 0), stop=(j == FC - 1)),
                            reads=[wb, actb[j]], writes=[banks[q][1]])
            for q in range(4):
                dc = og * 4 + q
                hr, hrb = hres.next()
                S.dma("sp", hr[:], hv[:, dc, t0:t0 + TT], reads=[hTb], writes=[hrb])
                hn, hnb = hnew.next()
                S.op("dve", lambda e, hn=hn, hr=hr, pt=banks[q][0]: e.scalar_tensor_tensor(
                    out=hn[:], in0=pt[:], scalar=0.5, in1=hr[:], op0=ALU.mult, op1=ALU.add),
                    reads=[banks[q][1], hrb], writes=[hnb])
                S.dma("act", hv[:, dc, t0:t0 + TT], hn[:], reads=[hnb], writes=[hTb], local=False)


def emit_final(nc, S, st, C, hT, hTb, ntok, D, nw, y_tok, yb):
    DC = D // 128
    uT = sbt(nc, st, "o_uT", [128, DC, TT], F32)
    uTb = [Buf("ouT%d" % i) for i in range(DC)]
    wcol, wcolb = load_cols(nc, st, S, "o_nw", nw, DC)
    hb_ring = Ring(nc, st, "o_hb", 2, [128, 2, TT], F32)
    sq_ring = Ring(nc, st, "o_sq", 2, [128, TT], F32)
    rt = sbt(nc, st, "o_rt", [128, TT], F32)
    rstd = sbt(nc, st, "o_rstd", [128, TT], F32)
    rs = (rt, Buf("rt"), rstd, Buf("rstd"))
    ss_ps = pst(nc, st, "o_ss", [128, TT], F32)
    ss_b = Buf("ss")
    ps = PRing(nc, st, "o_ps", 4)
    yo = Ring(nc, st, "o_y", 2, [128, D], F32)
    for tile in range(ntok // TT):
        t0 = tile * TT
        emit_norm_to_uT(nc, S, st, C, hT, hTb, t0, uT, uTb, wcol, wcolb, rs, D, hb_ring, sq_ring, ss_ps, ss_b)
        for tb in range(TT // 128):
            yt, ytb = yo.next()
            for g in range(DC // 4):
                pt, pb = ps.next()
                for a in range(4):
                    dc = g * 4 + a
                    S.op("pe", lambda e, pt=pt, a=a, dc=dc, tb=tb: e.transpose(
                        out=pt[:, a * 128:(a + 1) * 128], in_=uT[:, dc, tb * 128:(tb + 1) * 128],
                        identity=C.ident[:]), reads=[uTb[dc], C.b], writes=[pb])
                if g % 2 == 0:
                    S.op("act", lambda e, pt=pt, yt=yt, g=g: e.copy(out=yt[:, g * 512:(g + 1) * 512], in_=pt[:]),
                         reads=[pb], writes=[ytb])
                else:
                    S.op("dve", lambda e, pt=pt, yt=yt, g=g: e.tensor_copy(out=yt[:, g * 512:(g + 1) * 512], in_=pt[:]),
                         reads=[pb], writes=[ytb])
            S.dma("act", y_tok[t0 + tb * 128:t0 + (tb + 1) * 128, :], yt[:], reads=[ytb], writes=[yb], local=False)


def emit_norm_store(nc, S, st, C, hT, hTb, ntok, D, nw, uT_dram, uT_dramb):
    DC = D // 128
    uTr = Ring(nc, st, "n_uT", 2, [128, DC, TT], BF16)
    wcol, wcolb = load_cols(nc, st, S, "n_nw", nw, DC)
    hb_ring = Ring(nc, st, "n_hb", 2, [128, 2, TT], F32)
    sq_ring = Ring(nc, st, "n_sq", 2, [128, TT], F32)
    rt = sbt(nc, st, "n_rt", [128, TT], F32)
    rstd = sbt(nc, st, "n_rstd", [128, TT], F32)
    rs = (rt, Buf("rt"), rstd, Buf("rstd"))
    ss_ps = pst(nc, st, "n_ss", [128, TT], F32)
    ss_b = Buf("ss")
    uv = uT_dram.rearrange("(c p) t -> p c t", p=128)
    for tile in range(ntok // TT):
        t0 = tile * TT
        uT, ub = uTr.next()
        uTb = [ub] * DC
        emit_norm_to_uT(nc, S, st, C, hT, hTb, t0, uT, uTb, wcol, wcolb, rs, D, hb_ring, sq_ring, ss_ps, ss_b)
        S.dma("act", uv[:, :, t0:t0 + TT], uT[:], reads=[ub], writes=[uT_dramb], local=False)


def emit_outproj(nc, S, st, C, hT, hTb, ntok, D, o_view, o_b, w_out):
    DC = D // 128
    NT = ntok // TT
    oT = sbt(nc, st, "p_oT", [128, DC, ntok], BF16)
    oTb = Buf("p_oT")
    for kc in range(0, DC, min(8, DC)):
        S.dma("sp", oT[:, kc:kc + min(8, DC), :], o_view[:, kc:kc + min(8, DC), :], reads=[o_b], writes=[oTb])
    wr = Ring(nc, st, "p_w", 3, [128, DC, 128], BF16)
    hres = Ring(nc, st, "p_hres", 2, [128, TT], F32)
    hnew = Ring(nc, st, "p_hnew", 2, [128, TT], F32)
    po = PRing(nc, st, "p_po", 4)
    wv = w_out.rearrange("(c p) f -> p c f", p=128)
    hv = hT.rearrange("(c p) t -> p c t", p=128)
    slots = {}

    def pre(dc):
        if dc < DC and dc not in slots:
            t, b = wr.next()
            S.dma("pool", t[:], wv[:, :, dc * 128:(dc + 1) * 128], writes=[b])
            slots[dc] = (t, b)
    pre(0)
    pre(1)
    for dc in range(DC):
        pre(dc + 2)
        wt, wb = slots[dc]
        for ti in range(NT):
            pt, pb = po.next()
            for kc in range(DC):
                S.op("pe", lambda e, pt=pt, wt=wt, kc=kc, ti=ti: e.matmul(
                    pt[:], lhsT=wt[:, kc, :], rhs=oT[:, kc, ti * TT:(ti + 1) * TT],
                    start=(kc == 0), stop=(kc == DC - 1)), reads=[wb, oTb], writes=[pb])
            hr, hrb = hres.next()
            S.dma("sp", hr[:], hv[:, dc, ti * TT:(ti + 1) * TT], reads=[hTb], writes=[hrb])
            hn, hnb = hnew.next()
            S.op("dve", lambda e, hn=hn, hr=hr, pt=pt: e.tensor_tensor(out=hn[:], in0=pt[:], in1=hr[:], op=ALU.add),
                 reads=[pb, hrb], writes=[hnb])
            S.dma("act", hv[:, dc, ti * TT:(ti + 1) * TT], hn[:], reads=[hnb], writes=[hTb], local=False)


def emit_hgrn(nc, S, st, C, u_tile, u_b, NTOK, D, NH, w_in_c, lbl_c, gn, cmask_d, tri_d, oT_c, oT_cb):
    DC = D // 128
    CH = 64
    NCH = TT // CH
    wv = w_in_c.rearrange("(c p) f -> p c f", p=128)
    cmask = sbt(nc, st, "h_cmask", [128, TT], F32)
    tri = sbt(nc, st, "h_tri", [128, 4, 64], F32)
    identb = sbt(nc, st, "h_identb", [128, 128], BF16)
    cb = Buf("h_consts")
    S.dma("sp", cmask[:], cmask_d, writes=[cb])
    S.dma("sp", tri[:], tri_d.rearrange("p (a t) -> p a t", a=4), writes=[cb])
    S.op("dve", lambda e: e.tensor_copy(out=identb[:], in_=C.ident[:]), reads=[C.b], writes=[cb])
    gcol = sbt(nc, st, "h_gn", [128, 1], F32)
    S.dma("sp", gcol[:], gn.rearrange("(p o) -> p o", o=1), writes=[cb])
    lg = sbt(nc, st, "h_lg", [128, 3, NH], F32)
    S.dma("sp", lg[:], lbl_c.rearrange("r (h p) -> p r h", p=128), writes=[cb], allow_slow_non_contiguous=True)
    le = sbt(nc, st, "h_le", [128, 3, NH], F32)
    lsum = sbt(nc, st, "h_lsum", [128, NH], F32)
    lb = sbt(nc, st, "h_lb", [128, NH], F32)
    oml = sbt(nc, st, "h_oml", [128, NH], F32)
    lbb = Buf("h_lb")
    S.op("act", lambda e: e.activation(out=le[:], in_=lg[:], func=AF.Exp), reads=[cb], writes=[lbb])
    S.op("dve", lambda e: e.tensor_tensor(out=lsum[:], in0=le[:, 0, :], in1=le[:, 1, :], op=ALU.add), reads=[lbb], writes=[lbb])
    S.op("dve", lambda e: e.tensor_tensor(out=lsum[:], in0=lsum[:], in1=le[:, 2, :], op=ALU.add), reads=[lbb], writes=[lbb])
    S.op("dve", lambda e: e.reciprocal(out=lsum[:], in_=lsum[:]), reads=[lbb], writes=[lbb])
    S.op("dve", lambda e: e.tensor_tensor(out=lb[:], in0=le[:, 0, :], in1=lsum[:], op=ALU.mult), reads=[lbb], writes=[lbb])
    S.op("dve", lambda e: e.tensor_scalar(out=oml[:], in0=lb[:], scalar1=-1.0, scalar2=1.0, op0=ALU.mult, op1=ALU.add),
         reads=[lbb], writes=[lbb])
    wr = Ring(nc, st, "h_w", 2, [128, DC, 4 * 128], BF16)
    ur = Ring(nc, st, "h_u", 2, [128, DC, TT], BF16)

    def f32t(name):
        return sbt(nc, st, name, [128, TT], F32), Buf(name)
    qs, qsb = f32t("h_qs")
    sg, sgb = f32t("h_sg")
    fg, fgb = f32t("h_fg")
    kk, kkb = f32t("h_kk")
    gl, glb = f32t("h_gl")
    bb, bbb = f32t("h_bb")
    eb, ebb = f32t("h_eb")
    enb, enbb = f32t("h_enb")
    kd32, kd32b = f32t("h_kd32")
    gs, gsb = f32t("h_gs")
    osb, osbb = f32t("h_osb")
    osq, osqb = f32t("h_osq")
    rt, rtb = f32t("h_rt")
    rstd, rstdb = f32t("h_rstd")
    t1, t1b = f32t("h_t1")
    qd = sbt(nc, st, "h_qd", [128, TT], BF16); qdb = Buf("qd")
    kdi = sbt(nc, st, "h_kdi", [128, TT], BF16); kdib = Buf("kdi")
    kdec = sbt(nc, st, "h_kdec", [128, TT], F32); kdecb = Buf("kdec")
    ib = sbt(nc, st, "h_ib", [128, TT], F32); ibb = Buf("ib")
    vtok = sbt(nc, st, "h_vtok", [128, 4, 128], BF16); vtokb = Buf("vtok")
    ktok = sbt(nc, st, "h_ktok", [128, 4, 128], BF16); ktokb = Buf("ktok")
    am = sbt(nc, st, "h_am", [128, 4, 2, 64], BF16); amb = Buf("am")
    oo_r = Ring(nc, st, "h_oo", 2, [128, TT], BF16)
    S32 = sbt(nc, st, "h_S32", [128, 128], F32); S32b = Buf("S32")
    Sbf_r = Ring(nc, st, "h_Sbf", 2, [128, 128], BF16)
    pq = pst(nc, st, "h_pq", [128, TT], F32); pqb = Buf("pq")
    pf = pst(nc, st, "h_pf", [128, TT], F32); pfb = Buf("pf")
    pi = pst(nc, st, "h_pi", [128, TT], F32); pib = Buf("pi")
    pgt = pst(nc, st, "h_pg", [128, TT], F32); pgb = Buf("pg")
    pa = pst(nc, st, "h_pa", [128, TT], F32); pab = Buf("pa")
    po = pst(nc, st, "h_po", [128, TT], F32); pob = Buf("po")
    pS = pst(nc, st, "h_pS", [128, TT], F32); pSb = Buf("pS")
    pa4 = pa[:].rearrange("p (a h t) -> p a h t", a=4, h=2)
    ov = oT_c.rearrange("(h p) t -> p h t", p=128)

    NT = NTOK // TT
    wslots = {}

    def prew(hl):
        if hl < NH and hl not in wslots:
            t, b = wr.next()
            for kind in range(4):
                col = kind * NH * 128 + hl * 128
                S.dma("pool", t[:, :, kind * 128:(kind + 1) * 128], wv[:, :, col:col + 128], writes=[b])
            wslots[hl] = (t, b)
    uslots = {}

    def preu(idx):
        if idx < NH * NT and idx not in uslots:
            t, b = ur.next()
            t0 = (idx % NT) * TT
            for kc in range(0, DC, min(8, DC)):
                S.dma("sp", t[:, kc:kc + min(8, DC), :], u_tile(t0)[:, kc:kc + min(8, DC), :], reads=[u_b], writes=[b])
            uslots[idx] = (t, b)
    prew(0)
    preu(0)
    for hl in range(NH):
        prew(hl + 1)
        wt, wb = wslots[hl]
        S.op("dve", lambda e: e.memset(S32[:], 0.0), writes=[S32b])
        Sbf, Sbfb = Sbf_r.next()
        S.op("dve", lambda e, Sbf=Sbf: e.memset(Sbf[:], 0.0), writes=[Sbfb])
        for ti in range(NT):
            idx = hl * NT + ti
            preu(idx + 1)
            ut, utb = uslots.pop(idx)
            t0 = ti * TT
            for kind, (pt, pb) in enumerate(((pq, pqb), (pf, pfb), (pi, pib), (pgt, pgb))):
                for kc in range(DC):
                    S.op("pe", lambda e, pt=pt, kind=kind, kc=kc, ut=ut, wt=wt: e.matmul(
                        pt[:], lhsT=wt[:, kc, kind * 128:(kind + 1) * 128], rhs=ut[:, kc, :],
                        start=(kc == 0), stop=(kc == DC - 1)), reads=[wb, utb], writes=[pb])
            S.op("act", lambda e: e.activation(out=qs[:], in_=pq[:], func=AF.Silu), reads=[pqb], writes=[qsb])
            S.op("act", lambda e: e.activation(out=gs[:], in_=pgt[:], func=AF.Silu), reads=[pgb], writes=[gsb])
            S.op("act", lambda e: e.activation(out=sg[:], in_=pf[:], func=AF.Sigmoid), reads=[pfb], writes=[sgb])
            S.op("act", lambda e: e.copy(out=ib[:], in_=pi[:]), reads=[pib], writes=[ibb])
            S.op("dve", lambda e, hl=hl: e.tensor_scalar(out=fg[:], in0=sg[:], scalar1=oml[:, hl:hl + 1],
                                                         scalar2=lb[:, hl:hl + 1], op0=ALU.mult, op1=ALU.add),
                 reads=[sgb, lbb], writes=[fgb])
            S.op("dve", lambda e: e.tensor_scalar(out=kk[:], in0=fg[:], scalar1=-1.0, scalar2=1.0,
                                                  op0=ALU.mult, op1=ALU.add), reads=[fgb], writes=[kkb])
            S.op("act", lambda e: e.activation(out=gl[:], in_=fg[:], func=AF.Ln), reads=[fgb], writes=[glb])
            S.op("dve", lambda e: e.tensor_tensor_scan(out=bb[:], data0=cmask[:], data1=gl[:], initial=0.0,
                                                       op0=ALU.mult, op1=ALU.add), reads=[glb, cb], writes=[bbb])
            S.op("act", lambda e: e.activation(out=eb[:], in_=bb[:], func=AF.Exp), reads=[bbb], writes=[ebb])
            S.op("act", lambda e: e.activation(out=enb[:], in_=bb[:], func=AF.Exp, scale=-1.0), reads=[bbb], writes=[enbb])
            S.op("dve", lambda e: e.tensor_tensor(out=qd[:], in0=qs[:], in1=eb[:], op=ALU.mult), reads=[qsb, ebb], writes=[qdb])
            S.op("dve", lambda e: e.tensor_tensor(out=kd32[:], in0=kk[:], in1=enb[:], op=ALU.mult), reads=[kkb, enbb], writes=[kd32b])
            S.op("act", lambda e: e.copy(out=kdi[:], in_=kd32[:]), reads=[kd32b], writes=[kdib])
            for c in range(NCH):
                last = c * CH + CH - 1
                S.op("dve", lambda e, c=c, last=last: e.tensor_scalar(
                    out=kdec[:, c * CH:(c + 1) * CH], in0=kd32[:, c * CH:(c + 1) * CH],
                    scalar1=eb[:, last:last + 1], scalar2=None, op0=ALU.mult), reads=[kd32b, ebb], writes=[kdecb])
            for a in range(4):
                S.op("pe", lambda e, a=a: e.transpose(out=pi[:, a * 128:(a + 1) * 128], in_=ib[:, a * 128:(a + 1) * 128], identity=C.ident[:]),
                     reads=[ibb, C.b], writes=[pib])
            for a in range(4):
                S.op("pe", lambda e, a=a: e.transpose(out=pf[:, a * 128:(a + 1) * 128], in_=kdec[:, a * 128:(a + 1) * 128], identity=C.ident[:]),
                     reads=[kdecb, C.b], writes=[pfb])
            S.op("act", lambda e: e.copy(out=vtok[:], in_=pi[:].rearrange("p (a t) -> p a t", a=4)), reads=[pib], writes=[vtokb])
            S.op("dve", lambda e: e.tensor_copy(out=ktok[:], in_=pf[:].rearrange("p (a t) -> p a t", a=4)), reads=[pfb], writes=[ktokb])
            for c in range(NCH):
                h = c % 2
                S.op("pe", lambda e, c=c, h=h: e.matmul(
                    pa[64 * h:64 * h + 64, c * CH:(c + 1) * CH], lhsT=kdi[:, c * CH:(c + 1) * CH],
                    rhs=qd[:, c * CH:(c + 1) * CH], start=True, stop=True), reads=[kdib, qdb], writes=[pab])
            for h in range(2):
                S.op("dve", lambda e, h=h: e.tensor_tensor(
                    out=am[64 * h:64 * h + 64, :, h, :], in0=pa4[64 * h:64 * h + 64, :, h, :],
                    in1=tri[64 * h:64 * h + 64, :, :], op=ALU.mult), reads=[pab, cb], writes=[amb])
            for c in range(NCH):
                h = c % 2
                a = c // 2
                last = c * CH + CH - 1
                S.op("pe", lambda e, c=c, h=h, a=a: e.matmul(
                    po[:, c * CH:(c + 1) * CH], lhsT=vtok[64 * h:64 * h + 64, a, :],
                    rhs=am[64 * h:64 * h + 64, a, h, :], start=True, stop=False), reads=[vtokb, amb], writes=[pob])
                S.op("pe", lambda e, c=c, Sbf=Sbf: e.matmul(
                    po[:, c * CH:(c + 1) * CH], lhsT=Sbf[:], rhs=qd[:, c * CH:(c + 1) * CH],
                    start=False, stop=True), reads=[Sbfb, qdb], writes=[pob])
                S.op("pe", lambda e, h=h, a=a: e.matmul(
                    pS[:, 0:128], lhsT=ktok[64 * h:64 * h + 64, a, :], rhs=vtok[64 * h:64 * h + 64, a, :],
                    start=True, stop=True), reads=[ktokb, vtokb], writes=[pSb])
                S.op("dve", lambda e, last=last: e.scalar_tensor_tensor(
                    out=S32[:], in0=S32[:], scalar=eb[:, last:last + 1], in1=pS[:, 0:128],
                    op0=ALU.mult, op1=ALU.add), reads=[S32b, ebb, pSb], writes=[S32b])
                Sbf, Sbfb = Sbf_r.next()
                S.op("act", lambda e, Sbf=Sbf: e.copy(out=Sbf[:], in_=S32[:]), reads=[S32b], writes=[Sbfb])
            S.op("act", lambda e: e.copy(out=osb[:], in_=po[:]), reads=[pob], writes=[osbb])
            S.op("act", lambda e: e.activation(out=osq[:], in_=po[:], func=AF.Square), reads=[pob], writes=[osqb])
            S.op("pe", lambda e: e.matmul(pa[:], lhsT=C.ones[:], rhs=osq[:], start=True, stop=True),
                 reads=[osqb, C.b], writes=[pab])
            S.op("act", lambda e: e.activation(out=rt[:], in_=pa[:], func=AF.Sqrt, bias=C.eps[:], scale=1.0 / 128),
                 reads=[pab, C.b], writes=[rtb])
            S.op("dve", lambda e: e.reciprocal(out=rstd[:], in_=rt[:]), reads=[rtb], writes=[rstdb])
            S.op("dve", lambda e: e.scalar_tensor_tensor(out=t1[:], in0=osb[:], scalar=gcol[:, 0:1], in1=rstd[:],
                                                         op0=ALU.mult, op1=ALU.mult), reads=[osbb, rstdb, cb], writes=[t1b])
            oo, oob = oo_r.next()
            S.op("dve", lambda e, oo=oo: e.tensor_tensor(out=oo[:], in0=t1[:], in1=gs[:], op=ALU.mult),
                 reads=[t1b, gsb], writes=[oob])
            S.dma("act", ov[:, hl, t0:t0 + TT], oo[:], reads=[oob], writes=[oT_cb], local=False)


def emit_sb(nc, S, st, C, u_tile, u_b, NTOK, D, NH, w_in_c, trineg_d, mval_d, oT_c, oT_cb):
    DC = D // 128
    scale = 1.0 / math.sqrt(128.0)
    NT = NTOK // TT
    NB = NTOK // 128
    wv = w_in_c.rearrange("(c p) f -> p c f", p=128)
    ov = oT_c.rearrange("(h p) t -> p h t", p=128)
    trineg = sbt(nc, st, "s_trineg", [128, 128], F32)
    onesneg = sbt(nc, st, "s_onesneg", [128, 128], F32)
    mval = sbt(nc, st, "s_mval", [128, 4, TT], F32)
    cb = Buf("s_consts")
    S.dma("sp", trineg[:], trineg_d, writes=[cb])
    S.dma("sp", mval[:], mval_d.rearrange("p (m t) -> p m t", m=4), writes=[cb])
    S.op("dve", lambda e: e.memset(onesneg[:], -1.0), writes=[cb])
    wr = Ring(nc, st, "s_w", 2, [128, DC, 3 * 128], BF16)
    ur = Ring(nc, st, "s_u", 2, [128, DC, TT], BF16)
    qT = sbt(nc, st, "s_qT", [128, NTOK], BF16); qTb = Buf("qT")
    kT = sbt(nc, st, "s_kT", [128, NTOK], BF16); kTb = Buf("kT")
    vtok = sbt(nc, st, "s_vtok", [128, NB, 128], BF16); vtokb = Buf("vtok")
    v32 = sbt(nc, st, "s_v32", [128, TT], F32); v32b = Buf("v32")
    E_r = Ring(nc, st, "s_E", 2, [128, TT], F32)
    P_r = Ring(nc, st, "s_P", 3, [128, TT], F32)
    X_r = Ring(nc, st, "s_X", 2, [128, TT], F32)
    AT_r = Ring(nc, st, "s_AT", 3, [128, TT], BF16)
    Cs = sbt(nc, st, "s_Cs", [128, TT], F32); Csb = Buf("Cs")
    oo_r = Ring(nc, st, "s_oo", 2, [128, TT], BF16)
    pq = pst(nc, st, "s_pq", [128, TT], F32); pqb = Buf("pq")
    pk = pst(nc, st, "s_pk", [128, TT], F32); pkb = Buf("pk")
    pvv = pst(nc, st, "s_pv", [128, TT], F32); pvb = Buf("pv")
    Z_r = PRing(nc, st, "s_Z", 2)
    A_r = PRing(nc, st, "s_A", 2)
    pO = pst(nc, st, "s_pO", [128, TT], F32); pOb = Buf("pO")

    wslots = {}

    def prew(hl):
        if hl < NH and hl not in wslots:
            t, b = wr.next()
            for kind in range(3):
                col = kind * NH * 128 + hl * 128
                S.dma("pool", t[:, :, kind * 128:(kind + 1) * 128], wv[:, :, col:col + 128], writes=[b])
            wslots[hl] = (t, b)
    uslots = {}

    def preu(idx):
        if idx < NH * NT and idx not in uslots:
            t, b = ur.next()
            t0 = (idx % NT) * TT
            for kc in range(0, DC, min(8, DC)):
                S.dma("sp", t[:, kc:kc + min(8, DC), :], u_tile(t0)[:, kc:kc + min(8, DC), :], reads=[u_b], writes=[b])
            uslots[idx] = (t, b)
    prew(0)
    preu(0)
    for hl in range(NH):
        prew(hl + 1)
        wt, wb = wslots[hl]
        for ti in range(NT):
            idx = hl * NT + ti
            preu(idx + 1)
            ut, utb = uslots.pop(idx)
            t0 = ti * TT
            for kind, (pt, pb) in enumerate(((pq, pqb), (pk, pkb), (pvv, pvb))):
                for kc in range(DC):
                    S.op("pe", lambda e, pt=pt, kind=kind, kc=kc, ut=ut, wt=wt: e.matmul(
                        pt[:], lhsT=wt[:, kc, kind * 128:(kind + 1) * 128], rhs=ut[:, kc, :],
                        start=(kc == 0), stop=(kc == DC - 1)), reads=[wb, utb], writes=[pb])
            S.op("act", lambda e, t0=t0: e.mul(out=qT[:, t0:t0 + TT], in_=pq[:], mul=scale), reads=[pqb], writes=[qTb])
            S.op("dve", lambda e, t0=t0: e.tensor_copy(out=kT[:, t0:t0 + TT], in_=pk[:]), reads=[pkb], writes=[kTb])
            S.op("act", lambda e: e.copy(out=v32[:], in_=pvv[:]), reads=[pvb], writes=[v32b])
            for a in range(4):
                S.op("pe", lambda e, a=a: e.transpose(out=pvv[:, a * 128:(a + 1) * 128], in_=v32[:, a * 128:(a + 1) * 128],
                                                      identity=C.ident[:]), reads=[v32b, C.b], writes=[pvb])
            S.op("dve", lambda e, ti=ti: e.tensor_copy(out=vtok[:, 4 * ti:4 * ti + 4, :],
                                                       in_=pvv[:].rearrange("p (a t) -> p a t", a=4)),
                 reads=[pvb], writes=[vtokb])
        for qt in range(NT):
            t0 = qt * TT
            blocks = list(range(4 * qt + 3, -1, -1))
            n = len(blocks)
            Pn = [None] * n
            An = [None] * n
            for step in range(n + 2):
                if step < n:
                    i = step
                    kb = blocks[i]
                    zt, zb = Z_r.next()
                    S.op("pe", lambda e, zt=zt, kb=kb, t0=t0: e.matmul(
                        zt[:], lhsT=kT[:, kb * 128:(kb + 1) * 128], rhs=qT[:, t0:t0 + TT], start=True, stop=True),
                        reads=[kTb, qTb], writes=[zb])
                    et, etb = E_r.next()
                    S.op("act", lambda e, et=et, zt=zt: e.activation(out=et[:], in_=zt[:], func=AF.Exp), reads=[zb], writes=[etb])
                    pt_, ptb_ = P_r.next()
                    S.op("act", lambda e, et=et, pt_=pt_: e.activation(out=pt_[:], in_=et[:], func=AF.Ln, bias=1.0),
                         reads=[etb], writes=[ptb_])
                    if kb >= 4 * qt:
                        m = kb - 4 * qt
                        S.op("dve", lambda e, pt_=pt_, m=m: e.tensor_tensor(out=pt_[:], in0=pt_[:], in1=mval[:, m, :], op=ALU.mult),
                             reads=[ptb_, cb], writes=[ptb_])
                    Pn[i] = (pt_, ptb_)
                if 1 <= step <= n:
                    i = step - 1
                    kb = blocks[i]
                    pt_, ptb_ = Pn[i]
                    at, ab = A_r.next()
                    S.op("pe", lambda e, at=at, kb=kb, t0=t0: e.matmul(
                        at[:], lhsT=kT[:, kb * 128:(kb + 1) * 128], rhs=qT[:, t0:t0 + TT], start=True, stop=False),
                        reads=[kTb, qTb], writes=[ab])
                    S.op("pe", lambda e, at=at, pt_=pt_, i=i: e.matmul(
                        at[:], lhsT=trineg[:], rhs=pt_[:], start=False, stop=(i == 0)), reads=[ptb_, cb], writes=[ab])
                    if i > 0:
                        S.op("pe", lambda e, at=at: e.matmul(at[:], lhsT=onesneg[:], rhs=Cs[:], start=False, stop=True),
                             reads=[Csb, cb], writes=[ab])
                    if i < n - 1:
                        if i == 0:
                            S.op("pool", lambda e, pt_=pt_: e.tensor_copy(out=Cs[:], in_=pt_[:]), reads=[ptb_], writes=[Csb])
                        else:
                            S.op("pool", lambda e, pt_=pt_: e.tensor_tensor(out=Cs[:], in0=Cs[:], in1=pt_[:], op=ALU.add),
                                 reads=[ptb_, Csb], writes=[Csb])
                    aT, aTb = AT_r.next()
                    if kb >= 4 * qt:
                        m = kb - 4 * qt
                        xt, xb = X_r.next()
                        S.op("act", lambda e, xt=xt, at=at: e.activation(out=xt[:], in_=at[:], func=AF.Exp), reads=[ab], writes=[xb])
                        S.op("dve", lambda e, xt=xt, aT=aT, m=m: e.tensor_tensor(out=aT[:], in0=xt[:], in1=mval[:, m, :], op=ALU.mult),
                             reads=[xb, cb], writes=[aTb])
                    else:
                        S.op("act", lambda e, aT=aT, at=at: e.activation(out=aT[:], in_=at[:], func=AF.Exp), reads=[ab], writes=[aTb])
                    An[i] = (aT, aTb)
                if step >= 2:
                    i = step - 2
                    kb = blocks[i]
                    aT, aTb = An[i]
                    S.op("pe", lambda e, aT=aT, kb=kb, i=i, n=n: e.matmul(
                        pO[:], lhsT=vtok[:, kb, :], rhs=aT[:], start=(i == 0), stop=(i == n - 1)),
                        reads=[vtokb, aTb], writes=[pOb])
            oo, oob = oo_r.next()
            S.op("dve", lambda e, oo=oo: e.tensor_copy(out=oo[:], in_=pO[:]), reads=[pOb], writes=[oob])
            S.dma("sp", ov[:, hl, t0:t0 + TT], oo[:], reads=[oob], writes=[oT_cb], local=False)


D_MODEL = 4096
SEQ = 8192
D_FF = 11008
NCORES = 8
TPC = SEQ // NCORES
HPC = 4


def consts_np():
    cm = np.ones((128, 512), np.float32)
    cm[:, ::64] = 0
    s = np.arange(128)[:, None] % 64
    t = np.arange(64)[None, :]
    tri = np.tile((s <= t).astype(np.float32), (1, 4))
    j = np.arange(128)[:, None]
    s2 = np.arange(128)[None, :]
    trineg = -(j >= s2).astype(np.float32)
    p = np.arange(128)[:, None, None]
    m = np.arange(4)[None, :, None]
    tt = np.arange(512)[None, None, :]
    mval = ((128 * m + p) < tt).astype(np.float32).reshape(128, 4 * 512)
    return {"c_ones": np.ones((128, 128), np.float32), "c_ident": np.eye(128, dtype=np.float32),
            "c_eps": np.full((128, 1), EPS, np.float32), "c_cmask": cm, "c_tri": tri,
            "c_trineg": trineg, "c_mval": mval}


class Prog:
    def __init__(self):
        self.nc = bass.Bass("TRN2", target_bir_lowering=False)
        self.gst = contextlib.ExitStack()
        self.S = Sched(self.nc, self.gst)
        self.cd = {k[2:]: self.inp(k, v.shape) for k, v in consts_np().items()}

    def inp(self, name, shape, dt=F32):
        return self.nc.dram_tensor(name, list(shape), dt, kind="ExternalInput").ap()

    def out(self, name, shape, dt=F32):
        return self.nc.dram_tensor(name, list(shape), dt, kind="ExternalOutput").ap()

    def scratch(self, name, shape, dt=F32):
        return self.nc.dram_tensor(name, list(shape), dt, kind="Internal").ap()

    def stage(self, fn, final_bufs=None):
        with contextlib.ExitStack() as st:
            C = Consts(self.nc, st, self.S, self.cd)
            fn(self.nc, self.S, st, C)
            if final_bufs is not None:
                self.S.wait_all("sp", final_bufs)
            self.S.emit()

    def done(self):
        self.gst.close()
        return self.nc


def ffn_inputs(P, tag):
    return (P.inp(tag + "_nw", [D_MODEL]), P.inp(tag + "_wgu", [D_MODEL, 2 * D_FF]), P.inp(tag + "_wdn", [D_FF, D_MODEL]))


def copy_dram(S, dst, dstb, src, srcb, rows):
    step = rows // 8
    for r in range(0, rows, step):
        S.dma("sp", dst[r:r + step, :], src[r:r + step, :], reads=[srcb], writes=[dstb], local=False)


def build_L1():
    P = Prog()
    x = P.inp("x", [TPC, D_MODEL])
    f1 = ffn_inputs(P, "f1")
    mixnw = P.inp("mixnw", [D_MODEL])
    hT = P.out("hT", [D_MODEL, TPC])
    uT = P.out("uT", [D_MODEL, TPC], BF16)
    hTb, uTb = Buf("hT"), Buf("uT")
    P.stage(lambda nc, S, st, C: emit_transpose_in(nc, S, st, C, x, hT, hTb, TPC, D_MODEL))
    P.stage(lambda nc, S, st, C: emit_ffn(nc, S, st, C, hT, hTb, TPC, D_MODEL, D_FF, *f1))
    P.stage(lambda nc, S, st, C: emit_norm_store(nc, S, st, C, hT, hTb, TPC, D_MODEL, mixnw, uT, uTb), final_bufs=[hTb, uTb])
    return P.done()


def build_L2():
    P = Prog()
    uT = P.inp("uT_all", [D_MODEL, SEQ], BF16)
    w = P.inp("w_in_c", [D_MODEL, 4 * HPC * 128])
    lbl = P.inp("lbl_c", [3, HPC * 128])
    gn = P.inp("gn", [128])
    o = P.out("oT_c", [HPC * 128, SEQ], BF16)
    ub, ob = Buf("u"), Buf("o")
    uv = uT.rearrange("(c p) t -> p c t", p=128)
    P.stage(lambda nc, S, st, C: emit_hgrn(nc, S, st, C, lambda t0: uv[:, :, t0:t0 + TT], ub, SEQ, D_MODEL, HPC,
                                          w, lbl, gn, P.cd["cmask"], P.cd["tri"], o, ob), final_bufs=[ob])
    return P.done()


def build_L3():
    P = Prog()
    hin = P.inp("hT_in", [D_MODEL, TPC])
    o_c = P.inp("o_c", [D_MODEL, TPC], BF16)
    w_out = P.inp("w_out", [D_MODEL, D_MODEL])
    f2 = ffn_inputs(P, "f2")
    f1 = ffn_inputs(P, "f1")
    mixnw = P.inp("mixnw", [D_MODEL])
    hT = P.out("hT", [D_MODEL, TPC])
    uT = P.out("uT", [D_MODEL, TPC], BF16)
    hTb, uTb, hinb, ocb = Buf("hT"), Buf("uT"), Buf("hin"), Buf("oc")
    ovw = o_c.rearrange("(c p) t -> p c t", p=128)

    def s0(nc, S, st, C):
        copy_dram(S, hT, hTb, hin, hinb, D_MODEL)
        emit_outproj(nc, S, st, C, hT, hTb, TPC, D_MODEL, ovw, ocb, w_out)
    P.stage(s0)
    P.stage(lambda nc, S, st, C: emit_ffn(nc, S, st, C, hT, hTb, TPC, D_MODEL, D_FF, *f2))
    P.stage(lambda nc, S, st, C: emit_ffn(nc, S, st, C, hT, hTb, TPC, D_MODEL, D_FF, *f1))
    P.stage(lambda nc, S, st, C: emit_norm_store(nc, S, st, C, hT, hTb, TPC, D_MODEL, mixnw, uT, uTb), final_bufs=[hTb, uTb])
    return P.done()


def build_L4():
    P = Prog()
    uT = P.inp("uT_all", [D_MODEL, SEQ], BF16)
    w = P.inp("w_in_c", [D_MODEL, 3 * HPC * 128])
    o = P.out("oT_c", [HPC * 128, SEQ], BF16)
    ub, ob = Buf("u"), Buf("o")
    uv = uT.rearrange("(c p) t -> p c t", p=128)
    P.stage(lambda nc, S, st, C: emit_sb(nc, S, st, C, lambda t0: uv[:, :, t0:t0 + TT], ub, SEQ, D_MODEL, HPC,
                                        w, P.cd["trineg"], P.cd["mval"], o, ob), final_bufs=[ob])
    return P.done()


def build_L5():
    P = Prog()
    hin = P.inp("hT_in", [D_MODEL, TPC])
    o_c = P.inp("o_c", [D_MODEL, TPC], BF16)
    w_out = P.inp("w_out", [D_MODEL, D_MODEL])
    f2 = ffn_inputs(P, "f2")
    fnw = P.inp("fnw", [D_MODEL])
    hT = P.scratch("hT", [D_MODEL, TPC])
    y = P.out("y", [TPC, D_MODEL])
    hTb, yb, hinb, ocb = Buf("hT"), Buf("y"), Buf("hin"), Buf("oc")
    ovw = o_c.rearrange("(c p) t -> p c t", p=128)

    def s0(nc, S, st, C):
        copy_dram(S, hT, hTb, hin, hinb, D_MODEL)
        emit_outproj(nc, S, st, C, hT, hTb, TPC, D_MODEL, ovw, ocb, w_out)
    P.stage(s0)
    P.stage(lambda nc, S, st, C: emit_ffn(nc, S, st, C, hT, hTb, TPC, D_MODEL, D_FF, *f2))
    P.stage(lambda nc, S, st, C: emit_final(nc, S, st, C, hT, hTb, TPC, D_MODEL, fnw, y, yb), final_bufs=[yb])
    return P.done()


def _run(nc, in_maps):
    res = run_bass_kernel_spmd(nc, in_maps, core_ids=list(range(NCORES)))
    return res.results


def kernel(x, ffn1_norm, ffn1_w_gu, ffn1_w_down, mix_norm, ffn2_norm, ffn2_w_gu, ffn2_w_down,
           hgrn_w_in, hgrn_lb_logits, hgrn_gnorm, hgrn_w_out, sb_w_in, sb_w_out, final_norm):
    f32 = lambda a: np.ascontiguousarray(np.asarray(a, dtype=np.float32))
    x = f32(x)[0]
    cn = consts_np()
    n = NCORES
    W = HPC * 128

    def ffn_map(tag, nw, wgu, wdn):
        return {tag + "_nw": f32(nw), tag + "_wgu": f32(wgu), tag + "_wdn": f32(wdn)}

    maps = []
    for c in range(n):
        m = {"x": x[c * TPC:(c + 1) * TPC], "mixnw": f32(mix_norm[0])}
        m.update(ffn_map("f1", ffn1_norm[0], ffn1_w_gu[0], ffn1_w_down[0]))
        m.update(cn)
        maps.append(m)
    r1 = _run(build_L1(), maps)
    uT_all = np.concatenate([np.asarray(r["uT"]) for r in r1], axis=1)
    hw = f32(hgrn_w_in[0])
    maps = []
    for c in range(n):
        wc = np.concatenate([hw[:, k * D_MODEL + c * W:k * D_MODEL + (c + 1) * W] for k in range(4)], axis=1)
        m = {"uT_all": uT_all, "w_in_c": np.ascontiguousarray(wc),
             "lbl_c": f32(hgrn_lb_logits)[:, c * W:(c + 1) * W].copy(), "gn": f32(hgrn_gnorm[0])}
        m.update(cn)
        maps.append(m)
    r2 = _run(build_L2(), maps)
    o_all = np.concatenate([np.asarray(r["oT_c"]) for r in r2], axis=0)
    maps = []
    for c in range(n):
        m = {"hT_in": np.asarray(r1[c]["hT"]), "o_c": np.ascontiguousarray(o_all[:, c * TPC:(c + 1) * TPC]),
             "w_out": f32(hgrn_w_out[0]), "mixnw": f32(mix_norm[1])}
        m.update(ffn_map("f2", ffn2_norm[0], ffn2_w_gu[0], ffn2_w_down[0]))
        m.update(ffn_map("f1", ffn1_norm[1], ffn1_w_gu[1], ffn1_w_down[1]))
        m.update(cn)
        maps.append(m)
    r3 = _run(build_L3(), maps)
    uT_all = np.concatenate([np.asarray(r["uT"]) for r in r3], axis=1)
    sw = f32(sb_w_in[0])
    maps = []
    for c in range(n):
        wc = np.concatenate([sw[:, k * D_MODEL + c * W:k * D_MODEL + (c + 1) * W] for k in range(3)], axis=1)
        m = {"uT_all": uT_all, "w_in_c": np.ascontiguousarray(wc)}
        m.update(cn)
        maps.append(m)
    r4 = _run(build_L4(), maps)
    o_all = np.concatenate([np.asarray(r["oT_c"]) for r in r4], axis=0)
    maps = []
    for c in range(n):
        m = {"hT_in": np.asarray(r3[c]["hT"]), "o_c": np.ascontiguousarray(o_all[:, c * TPC:(c + 1) * TPC]),
             "w_out": f32(sb_w_out[0]), "fnw": f32(final_norm)}
        m.update(ffn_map("f2", ffn2_norm[1], ffn2_w_gu[1], ffn2_w_down[1]))
        m.update(cn)
        maps.append(m)
    r5 = _run(build_L5(), maps)
    y = np.concatenate([np.asarray(r["y"]) for r in r5], axis=0)
    return y[None].astype(np.float32)
```
